# Optimizing a Trainium2 kernel written in Bass

```python
import math
import jax
import jax.numpy as jnp
from jax import lax
import numpy as np

D_MODEL = 1024
BATCH = 1
SEQ = 16384
DEPTH = 4

GRID_W = 64
CTX_LEN = 256
HEAD_DIM = 64
ROPE_BASE = 10000.0
NORM_EPS = 1e-6
NEG_INF = -1e30
Q_BLOCK = 128
DIFF_HEADS = 4
DIFF_WIDTH = DIFF_HEADS * 2 * HEAD_DIM
NA_HEADS = 8
NA_WIDTH = NA_HEADS * HEAD_DIM
NA_MAX_ROWS = 8
NA_COLS = 16
AB_Q_COLS = DIFF_WIDTH + NA_WIDTH
AB_IN = 3 * DIFF_WIDTH + 3 * NA_WIDTH
AB_OUT = DIFF_WIDTH + NA_WIDTH
GQA_HEADS = 16
GQA_KV_HEADS = 4
GQA_GROUP = GQA_HEADS // GQA_KV_HEADS
SWA_WINDOW = 128
C_Q_COLS = GQA_HEADS * HEAD_DIM
C_KV_COLS = GQA_KV_HEADS * HEAD_DIM
C_IN = C_Q_COLS + 2 * C_KV_COLS
FFN_HIDDEN = (8 * D_MODEL + 3 * 256 - 1) // (3 * 256) * 256
N_EVEN = (DEPTH + 1) // 2
N_ODD = DEPTH // 2

kernel_name = "hybrid_diffattn_natten_swa_prefix_trunk"


def rmsnorm(x, g):
    xf = x.astype(jnp.float32)
    y = xf * lax.rsqrt(jnp.mean(xf * xf, axis=-1, keepdims=True) + NORM_EPS)
    return (y * g.astype(jnp.float32)).astype(x.dtype)


def modulate(h, shift, scale):
    return h * (1 + scale) + shift


def swiglu(h, w1, w3, w2):
    return (jax.nn.silu(h @ w1) * (h @ w3)) @ w2


def axial_rope_tables(n, dtype):
    t = jnp.arange(n)
    row = (t // GRID_W).astype(jnp.float32)
    col = (t % GRID_W).astype(jnp.float32)
    quarter = HEAD_DIM // 4
    inv = ROPE_BASE ** (-jnp.arange(quarter, dtype=jnp.float32) / quarter)
    ar = row[:, None] * inv
    ac = col[:, None] * inv
    ang = jnp.concatenate([ar, ar, ac, ac], axis=-1)
    return jnp.cos(ang).astype(dtype), jnp.sin(ang).astype(dtype)


def apply_rope(x, cos, sin):
    xs = x.reshape(x.shape[:-1] + (2, 2, HEAD_DIM // 4))
    rot = jnp.stack([-xs[..., 1, :], xs[..., 0, :]], axis=-2).reshape(x.shape)
    return x * cos[None, :, None, :] + rot * sin[None, :, None, :]


def heads(t, n_heads):
    return t.reshape(t.shape[:2] + (n_heads, -1))


def block_sweep(fn, q):
    B, S = q.shape[:2]
    nb = S // Q_BLOCK
    qb = jnp.moveaxis(q.reshape((B, nb, Q_BLOCK) + q.shape[2:]), 1, 0)
    starts = jnp.arange(nb) * Q_BLOCK
    out = lax.map(lambda a: fn(a[0], a[1]), (qb, starts))
    return jnp.moveaxis(out, 0, 1).reshape((B, S) + out.shape[3:])


def plain_attention(q, k, v):
    s = jnp.einsum('bqhd,bnhd->bhqn', q, k).astype(jnp.float32) * (HEAD_DIM ** -0.5)
    p = jax.nn.softmax(s, axis=-1).astype(v.dtype)
    return jnp.einsum('bhqn,bnhd->bqhd', p, v)


def diff_attend(q, k, v, lam_val):
    B, Q = q.shape[:2]
    s = jnp.einsum('bqhd,bnhd->bhqn', q, k).astype(jnp.float32) * (HEAD_DIM ** -0.5)
    p = jax.nn.softmax(s, axis=-1).reshape(B, DIFF_HEADS, 2, Q, -1)
    p = p[:, :, 0] - lam_val * p[:, :, 1]
    return jnp.einsum('bhqn,bnhd->bqhd', p.astype(v.dtype), v)


def diff_out(a, sub_g, lam_init):
    return (rmsnorm(a, sub_g) * (1.0 - lam_init)).reshape(a.shape[:2] + (-1,))


def neighbourhood_attention(q, k, v, k_c, v_c, rpb):
    B, S, _ = q.shape
    rows = S // GRID_W
    kr = min(NA_MAX_ROWS, rows)
    kc = NA_COLS
    grid = lambda t: t.reshape(B, rows, GRID_W, NA_HEADS, HEAD_DIM)
    qg, kg, vg = grid(q), grid(k), grid(v)
    kcx, vcx = heads(k_c, NA_HEADS), heads(v_c, NA_HEADS)
    col = jnp.arange(GRID_W)
    col_start = jnp.clip(col - kc // 2, 0, GRID_W - kc)
    col_idx = col_start[:, None] + jnp.arange(kc)
    dc_idx = col_idx - col[:, None] + (NA_COLS - 1)
    rpb_cols = rpb[:, :, dc_idx]
    scale = HEAD_DIM ** -0.5

    def row_fn(args):
        q_r, r = args
        rs = jnp.clip(r - kr // 2, 0, rows - kr)
        k_r = lax.dynamic_slice_in_dim(kg, rs, kr, axis=1)[:, :, col_idx]
        v_r = lax.dynamic_slice_in_dim(vg, rs, kr, axis=1)[:, :, col_idx]
        dr_idx = rs + jnp.arange(kr) - r + (NA_MAX_ROWS - 1)
        bias = jnp.transpose(rpb_cols[:, dr_idx], (0, 2, 1, 3))
        s_nb = jnp.einsum('bchd,bicjhd->bhcij', q_r, k_r).astype(jnp.float32) * scale + bias
        s_ctx = jnp.einsum('bchd,bnhd->bhcn', q_r, kcx).astype(jnp.float32) * scale
        s = jnp.concatenate([s_nb.reshape(B, NA_HEADS, GRID_W, kr * kc), s_ctx], axis=-1)
        p = jax.nn.softmax(s, axis=-1).astype(v.dtype)
        p_nb = p[..., :kr * kc].reshape(B, NA_HEADS, GRID_W, kr, kc)
        p_ctx = p[..., kr * kc:]
        return (jnp.einsum('bhcij,bicjhd->bchd', p_nb, v_r)
                + jnp.einsum('bhcn,bnhd->bchd', p_ctx, vcx))

    out = lax.map(row_fn, (jnp.moveaxis(qg, 1, 0), jnp.arange(rows)))
    return jnp.moveaxis(out, 0, 1).reshape(B, S, NA_WIDTH)


def mixer_ab(h_x, h_c, w_in, w_out, lam, sub_g, rpb, lam_init, cos, sin, with_ctx_out):
    B, S, _ = h_x.shape
    n_c = h_c.shape[1]
    cuts = [DIFF_WIDTH, AB_Q_COLS, AB_Q_COLS + DIFF_WIDTH, AB_Q_COLS + 2 * DIFF_WIDTH,
            AB_Q_COLS + 2 * DIFF_WIDTH + NA_WIDTH]
    aq_x, bq_x, ak_x, av_x, bk_x, bv_x = jnp.split(h_x @ w_in, cuts, axis=-1)
    if with_ctx_out:
        p_c = h_c @ w_in
        aq_c, bq_c, kv_c = p_c[..., :DIFF_WIDTH], p_c[..., DIFF_WIDTH:AB_Q_COLS], p_c[..., AB_Q_COLS:]
    else:
        kv_c = h_c @ w_in[:, AB_Q_COLS:]
    ak_c, av_c, bk_c, bv_c = jnp.split(kv_c, [DIFF_WIDTH, 2 * DIFF_WIDTH, 2 * DIFF_WIDTH + NA_WIDTH], axis=-1)

    lf = lam.astype(jnp.float32)
    lam_val = jnp.exp(jnp.sum(lf[0] * lf[1])) - jnp.exp(jnp.sum(lf[2] * lf[3])) + lam_init
    aq_xh = apply_rope(heads(aq_x, 2 * DIFF_HEADS), cos, sin)
    ak_xh = apply_rope(heads(ak_x, 2 * DIFF_HEADS), cos, sin)
    ak_ch, av_ch = heads(ak_c, 2 * DIFF_HEADS), heads(av_c, DIFF_HEADS)
    k_all = jnp.concatenate([ak_ch, ak_xh], axis=1)
    v_all = jnp.concatenate([av_ch, heads(av_x, DIFF_HEADS)], axis=1)
    a_x = block_sweep(lambda qb, start: diff_attend(qb, k_all, v_all, lam_val), aq_xh)
    a_x = diff_out(a_x, sub_g, lam_init)

    b_x = neighbourhood_attention(bq_x, bk_x, bv_x, bk_c, bv_c, rpb)
    out_x = jnp.concatenate([a_x, b_x], axis=-1) @ w_out

    out_c = None
    if with_ctx_out:
        a_c = diff_out(diff_attend(heads(aq_c, 2 * DIFF_HEADS), ak_ch, av_ch, lam_val), sub_g, lam_init)
        b_c = plain_attention(heads(bq_c, NA_HEADS), heads(bk_c, NA_HEADS),
                              heads(bv_c, NA_HEADS)).reshape(B, n_c, NA_WIDTH)
        out_c = jnp.concatenate([a_c, b_c], axis=-1) @ w_out
    return out_x, out_c


def gqa_sink_attend(q, k, v, mask, sinks):
    B, Q = q.shape[:2]
    qg = q.reshape(B, Q, GQA_KV_HEADS, GQA_GROUP, HEAD_DIM)
    s = jnp.einsum('bqkgd,bnkd->bkgqn', qg, k).astype(jnp.float32) * (HEAD_DIM ** -0.5)
    if mask is not None:
        s = jnp.where(mask, s, NEG_INF)
    sink = jnp.broadcast_to(sinks.astype(jnp.float32).reshape(GQA_KV_HEADS, GQA_GROUP, 1, 1),
                            (B, GQA_KV_HEADS, GQA_GROUP, Q, 1))
    p = jax.nn.softmax(jnp.concatenate([s, sink], axis=-1), axis=-1)[..., :-1]
    o = jnp.einsum('bkgqn,bnkd->bqkgd', p.astype(v.dtype), v)
    return o.reshape(B, Q, GQA_HEADS, HEAD_DIM)


def mixer_c(h_x, h_c, w_in, w_out, sinks, cos, sin, with_ctx_out):
    B, S, _ = h_x.shape
    q_x, k_x, v_x = jnp.split(h_x @ w_in, [C_Q_COLS, C_Q_COLS + C_KV_COLS], axis=-1)
    q_x = apply_rope(heads(q_x, GQA_HEADS), cos, sin)
    k_x = apply_rope(heads(k_x, GQA_KV_HEADS), cos, sin)
    v_x = heads(v_x, GQA_KV_HEADS)
    if with_ctx_out:
        p_c = h_c @ w_in
        q_c, kv_c = p_c[..., :C_Q_COLS], p_c[..., C_Q_COLS:]
    else:
        kv_c = h_c @ w_in[:, C_Q_COLS:]
    k_c, v_c = jnp.split(kv_c, 2, axis=-1)
    k_c, v_c = heads(k_c, GQA_KV_HEADS), heads(v_c, GQA_KV_HEADS)
    n_c = k_c.shape[1]

    pad = ((0, 0), (SWA_WINDOW, SWA_WINDOW), (0, 0), (0, 0))
    k_pad, v_pad = jnp.pad(k_x, pad), jnp.pad(v_x, pad)
    span = Q_BLOCK + 2 * SWA_WINDOW
    ctx_mask = jnp.ones((Q_BLOCK, n_c), dtype=bool)

    def blk(q_b, start):
        k_b = lax.dynamic_slice_in_dim(k_pad, start, span, axis=1)
        v_b = lax.dynamic_slice_in_dim(v_pad, start, span, axis=1)
        q_pos = start + jnp.arange(Q_BLOCK)
        k_pos = start - SWA_WINDOW + jnp.arange(span)
        band = ((k_pos >= 0) & (k_pos < S))[None, :] & (jnp.abs(q_pos[:, None] - k_pos[None, :]) <= SWA_WINDOW)
        return gqa_sink_attend(q_b, jnp.concatenate([k_c, k_b], axis=1),
                               jnp.concatenate([v_c, v_b], axis=1),
                               jnp.concatenate([ctx_mask, band], axis=-1), sinks)

    out_x = block_sweep(blk, q_x).reshape(B, S, C_Q_COLS) @ w_out
    out_c = None
    if with_ctx_out:
        o_c = gqa_sink_attend(heads(q_c, GQA_HEADS), k_c, v_c, None, sinks)
        out_c = o_c.reshape(B, n_c, C_Q_COLS) @ w_out
    return out_x, out_c


def setup_inputs(seed: int = 0) -> dict:
    key = jax.random.key(seed)
    ks = jax.random.split(key, 19)
    nrm = lambda k, shape, std: jax.random.normal(k, shape, jnp.float32) * std
    D = D_MODEL
    return {
        'x': nrm(ks[0], (BATCH, SEQ, D), 1.0),
        'c': nrm(ks[1], (BATCH, D), 1.0),
        'ctx': nrm(ks[2], (BATCH, CTX_LEN, D), 1.0),
        'c_ctx': nrm(ks[3], (D,), 1.0),
        'ada_w': nrm(ks[4], (DEPTH, D, 6 * D), 0.5 * D ** -0.5),
        'ada_b': nrm(ks[5], (DEPTH, 6 * D), 0.02),
        'norm_g': 1.0 + nrm(ks[6], (DEPTH, 2, D), 0.01),
        'w_in_ab': nrm(ks[7], (N_EVEN, D, AB_IN), D ** -0.5),
        'w_out_ab': nrm(ks[8], (N_EVEN, AB_OUT, D), AB_OUT ** -0.5),
        'diff_lambda': nrm(ks[9], (N_EVEN, 4, HEAD_DIM), 0.1),
        'diff_sub_g': 1.0 + nrm(ks[10], (N_EVEN, 2 * HEAD_DIM), 0.01),
        'na_rpb': nrm(ks[11], (N_EVEN, NA_HEADS, 2 * NA_MAX_ROWS - 1, 2 * NA_COLS - 1), 0.1),
        'w_in_c': nrm(ks[12], (N_ODD, D, C_IN), D ** -0.5),
        'w_out_c': nrm(ks[13], (N_ODD, C_Q_COLS, D), C_Q_COLS ** -0.5),
        'attn_sinks': nrm(ks[14], (N_ODD, GQA_HEADS), 0.5),
        'ffn_w1': nrm(ks[15], (DEPTH, D, FFN_HIDDEN), D ** -0.5),
        'ffn_w3': nrm(ks[16], (DEPTH, D, FFN_HIDDEN), D ** -0.5),
        'ffn_w2': nrm(ks[17], (DEPTH, FFN_HIDDEN, D), FFN_HIDDEN ** -0.5),
        'final_g': 1.0 + nrm(ks[18], (D,), 0.01),
    }


def reference(x, c, ctx, c_ctx, ada_w, ada_b, norm_g, w_in_ab, w_out_ab, diff_lambda, diff_sub_g,
              na_rpb, w_in_c, w_out_c, attn_sinks, ffn_w1, ffn_w3, ffn_w2, final_g):
    B, S, _ = x.shape
    cos, sin = axial_rope_tables(S, x.dtype)
    for layer in range(DEPTH):
        ctx_out = layer < DEPTH - 1
        mod_x = jnp.split((jax.nn.silu(c) @ ada_w[layer] + ada_b[layer])[:, None, :], 6, axis=-1)
        mod_c = jnp.split(jax.nn.silu(c_ctx) @ ada_w[layer] + ada_b[layer], 6, axis=-1)
        h_x = modulate(rmsnorm(x, norm_g[layer, 0]), mod_x[0], mod_x[1])
        h_c = modulate(rmsnorm(ctx, norm_g[layer, 0]), mod_c[0], mod_c[1])
        if layer % 2 == 0:
            e = layer // 2
            lam_init = 0.8 - 0.6 * math.exp(-0.3 * layer)
            o_x, o_c = mixer_ab(h_x, h_c, w_in_ab[e], w_out_ab[e], diff_lambda[e], diff_sub_g[e],
                                na_rpb[e], lam_init, cos, sin, ctx_out)
        else:
            o = layer // 2
            o_x, o_c = mixer_c(h_x, h_c, w_in_c[o], w_out_c[o], attn_sinks[o], cos, sin, ctx_out)
        x = x + mod_x[2] * o_x
        h_x = modulate(rmsnorm(x, norm_g[layer, 1]), mod_x[3], mod_x[4])
        x = x + mod_x[5] * swiglu(h_x, ffn_w1[layer], ffn_w3[layer], ffn_w2[layer])
        if ctx_out:
            ctx = ctx + mod_c[2] * o_c
            h_c = modulate(rmsnorm(ctx, norm_g[layer, 1]), mod_c[3], mod_c[4])
            ctx = ctx + mod_c[5] * swiglu(h_c, ffn_w1[layer], ffn_w3[layer], ffn_w2[layer])
    return rmsnorm(x, final_g)
```

```python
import contextlib
import os
import numpy as np
import ml_dtypes
import concourse.bass as bass
import concourse.mybir as mybir
from concourse.bass_utils import run_bass_kernel_spmd

F32, BF16 = mybir.dt.float32, mybir.dt.bfloat16
AF = mybir.ActivationFunctionType
ALU = mybir.AluOpType
AX = mybir.AxisListType

NC = 8
D = 1024
SEQ = 16384
OWN = SEQ // NC
CTX = 256
TOK = OWN + CTX
NT = TOK // 128
GROUPS = [(0, 512), (512, 512), (1024, 512), (1536, 512), (2048, 256)]
FFN = 2816
EPS = 1e-6
NEG = -1e30
WIN = 22


class Eng:
    def __init__(self, eng, sem, name):
        self.eng, self.sem, self.name = eng, sem, name
        self.cnt = 0
        self.seen = {}
        self.dslots = []
        self.di = 0


class Buf:
    __slots__ = ("w", "r")

    def __init__(self):
        self.w = {}
        self.r = {}


class P:
    def __init__(self):
        self.nc = bass.Bass("TRN2", target_bir_lowering=False)
        self.es = contextlib.ExitStack()
        nc = self.nc
        self.PE = self._eng(nc.tensor, "pe")
        self.ACT = self._eng(nc.scalar, "act")
        self.DVE = self._eng(nc.vector, "dve")
        self.POOL = self._eng(nc.gpsimd, "pool")
        self.SP = self._eng(nc.sync, "sp")
        self.engs = [self.PE, self.ACT, self.DVE, self.POOL, self.SP]
        for q, n in ((self.SP, 24), (self.POOL, 24)):
            for i in range(n):
                q.dslots.append([self.es.enter_context(nc.semaphore(f"d{q.name}{i}")), 0])
        self.bufs = {}
        self.banks = []
        for i in range(8):
            self.banks.append(self.es.enter_context(nc.psum_tensor(f"bank{i}", [128, 512], F32)))
        self.bank_i = 0

    def _eng(self, e, name):
        return Eng(e, self.es.enter_context(self.nc.semaphore("s_" + name)), name)

    def B(self, *key):
        b = self.bufs.get(key)
        if b is None:
            b = self.bufs[key] = Buf()
        return b

    def _wait(self, E, R, W):
        deps = []
        for b in R:
            deps.extend(b.w.values())
        for b in W:
            deps.extend(b.w.values())
            deps.extend(b.r.values())
        for (sem, val, owner) in deps:
            if owner is E and E is self.PE:
                continue
            if E.seen.get(id(sem), 0) >= val:
                continue
            E.eng.wait_ge(sem, val)
            E.seen[id(sem)] = val

    def op(self, E, fn, R=(), W=()):
        self._wait(E, R, W)
        ins = fn()
        E.cnt += 1
        ins.then_inc(E.sem, 1)
        ev = (E.sem, E.cnt, E)
        for b in R:
            b.r[id(E)] = ev
        for b in W:
            b.w = {id(E): ev}
            b.r = {}
        return ev

    def dma(self, Q, out, in_, R=(), W=(), slow=False):
        self._wait(Q, R, W)
        slot = Q.dslots[Q.di % len(Q.dslots)]
        Q.di += 1
        sem, cur = slot
        if cur > 0 and Q.seen.get(id(sem), 0) < cur:
            Q.eng.wait_ge(sem, cur)
            Q.seen[id(sem)] = cur
        if slow:
            Q.eng.dma_start(out=out, in_=in_, allow_slow_non_contiguous=True).then_inc(sem, 16)
        else:
            Q.eng.dma_start(out=out, in_=in_).then_inc(sem, 16)
        slot[1] = cur + 16
        ev = (sem, cur + 16, None)
        for b in R:
            b.r[("d", id(sem))] = ev
        for b in W:
            b.w = {("d", id(sem)): ev}
            b.r = {}
        return ev

    def barrier(self):
        evs = []
        for E in self.engs:
            if E.cnt > 0:
                evs.append((E.sem, E.cnt, E))
            for sem, cur in E.dslots:
                if cur > 0:
                    evs.append((sem, cur, None))
        for E in self.engs:
            for (sem, val, owner) in evs:
                if owner is E:
                    continue
                if E.seen.get(id(sem), 0) >= val:
                    continue
                E.eng.wait_ge(sem, val)
                E.seen[id(sem)] = val
        self.bufs = {}

    def bank(self):
        i = self.bank_i % 4
        self.bank_i += 1
        return self.banks[i], self.B("bank", i)

    def fbank(self, i):
        return self.banks[i], self.B("bank", i)

    def sb(self, ctx, name, shape, dt):
        self._uid = getattr(self, "_uid", 0) + 1
        return ctx.enter_context(self.nc.sbuf_tensor(f"{name}_{self._uid}", shape, dt))


def rot_perm(nheads):
    idx = []
    for h in range(nheads):
        for a in range(2):
            for s in range(2):
                for i in range(16):
                    idx.append(h * 64 + a * 32 + (1 - s) * 16 + i)
    return np.array(idx)


def rope_tables(core):
    t = core * OWN + np.arange(OWN)
    row = (t // 64).astype(np.float32)
    col = (t % 64).astype(np.float32)
    inv = (np.float32(10000.0) ** (-np.arange(16, dtype=np.float32) / np.float32(16))).astype(np.float32)
    ar = row[:, None] * inv
    ac = col[:, None] * inv
    ang = np.concatenate([ar, ar, ac, ac], axis=-1)
    cos = np.cos(ang).astype(np.float32)
    sin = np.sin(ang).astype(np.float32)
    sign = np.ones(64, np.float32)
    for a in range(2):
        sign[a * 32:a * 32 + 16] = -1.0
    sins = sin * sign[None, :]
    tab = np.zeros((128, 2, TOK), np.float32)
    tab[:, 0, OWN:] = 1.0
    for half in range(2):
        tab[half * 64:(half + 1) * 64, 0, :OWN] = cos.T
        tab[half * 64:(half + 1) * 64, 1, :OWN] = sins.T
    return tab


class Builder(P):
    def __init__(self, kind, even, last=False, dbg=False):
        super().__init__()
        self.kind, self.even, self.last, self.dbg = kind, even, last, dbg
        nc = self.nc
        di = lambda n, s, dt=F32: nc.dram_tensor(n, s, dt, kind="ExternalInput").ap()
        do = lambda n, s, dt=F32: nc.dram_tensor(n, s, dt, kind="ExternalOutput").ap()
        self.xin = di("xin", [TOK, D])
        if kind == "A":
            self.cvec = di("cvec", [2, D])
            self.ada_w = di("ada_w", [D, 6 * D])
            self.ada_b = di("ada_b", [6 * D])
            self.mod_o = do("mod_o", [2, 6 * D])
        self.norm_g = di("norm_g", [2, D])
        self.ident = di("ident", [128, 128])
        self.sel2 = di("sel2", [2, 258])
        self.cs = di("cs", [128, 2, TOK])
        self.win_cols = (3072 if even else 1536)
        self.w_in = di("w_in", [D, self.win_cols])
        self.w_rot = di("w_rot", [D, 1024 if even else 1280])
        self.KF = 1024 if even else 256
        if kind == "A":
            self.kT_o = do("kT_o", [self.KF, TOK], BF16)
            self.v_o = do("v_o", [TOK, self.KF], BF16)
        else:
            kfd = 512 if even else 256
            self.kT_g = di("kT_g", [NC, kfd, TOK], BF16)
            self.v_g = di("v_g", [NC, TOK, kfd], BF16)
            if even:
                self.nck = di("nck", [512, CTX], BF16)
                self.ncv = di("ncv", [CTX, 512], BF16)
            self.WF = 512 if even else 256
            self.win_kT = di("win_kT", [self.WF, WIN * 128], BF16)
            self.win_v = di("win_v", [WIN * 128, self.WF], BF16)
            self.w_out = di("w_out", [D, D])
            self.w1 = di("w1", [D, FFN])
            self.w3 = di("w3", [D, FFN])
            self.w2 = di("w2", [FFN, D])
            if even:
                self.lam = di("lam", [1, 256])
                self.subg = di("subg", [128, 1])
                self.nab = di("nab", [5, 128, 8 * 7 * 128])
            else:
                self.sinks = di("sinks", [1, 16])
                self.swam = di("swam", [128, 4 * 128])
            self.final_g = di("final_g", [D])
            self.lamc = di("lamc", [128, 2])
            self.xout = do("xout", [TOK, D])
            self.yout = do("yout", [OWN, D])
        if dbg:
            self.dbg_o = do("dbg_o", [128, 8, TOK], BF16)
            self.dbg2 = do("dbg2", [128, 2 * NT], F32)
            self.dbg3 = do("dbg3", [128, 64], F32)
            self.dbg4 = do("dbg4", [128, 96], F32)
            self.dbg5 = do("dbg5", [128, 16], F32)
            self.dbg6 = do("dbg6", [2, 6 * D], F32)
        if kind == "A":
            self.mod_d = nc.dram_tensor("mod_d", [2, 6 * D], F32).ap()
        else:
            self.mod_d = di("modin", [2, 6 * D])

    def mm(self, out, lhsT, rhs, start=True, stop=True):
        return self.nc.tensor.matmul(out, lhsT, rhs, start=start, stop=stop)

    def prologue(self, ps):
        nc = self.nc
        self.x = self.sb(ps, "x", [128, NT, D], F32)
        self.hT = self.sb(ps, "hT", [128, 8, TOK], BF16)
        self.idb = self.sb(ps, "idb", [128, 128], BF16)
        self.idf = self.sb(ps, "idf", [128, 128], F32)
        self.modT = self.sb(ps, "modT", [128, 48, 2], F32)
        self.gT = self.sb(ps, "gT", [128, 8, 2], F32)
        self.ab = self.sb(ps, "ab", [128, 4, 8, 2], F32)
        self.rstd = self.sb(ps, "rstd", [128, NT], F32)
        self.ss = self.sb(ps, "ss", [128, NT], F32)
        self.e2 = self.sb(ps, "e2", [2, 258], F32)
        self.e2b = self.sb(ps, "e2b", [2, 258], BF16)
        self.onesf = self.sb(ps, "onesf", [128, 128], F32)
        self.onesb = self.sb(ps, "onesb", [128, 128], BF16)
        self.selb = self.sb(ps, "selb", [128, 4], BF16)
        xv = self.xin.rearrange("(t p) d -> p t d", p=128)
        for t0 in range(0, NT, 3):
            self.dma(self.SP, self.x[:, t0:t0 + 3, :], xv[:, t0:t0 + 3, :], W=[self.B("x", t) for t in range(t0, t0 + 3)])
        self.dma(self.POOL, self.idb[:], self.ident, W=[self.B("idb")])
        self.dma(self.SP, self.idf[:], self.ident, W=[self.B("idf")])
        self.dma(self.SP, self.e2[:], self.sel2, W=[self.B("e2")])
        self.dma(self.POOL, self.e2b[:], self.sel2, W=[self.B("e2b")])
        self.op(self.DVE, lambda: nc.vector.memset(self.onesf[:], 1.0), W=[self.B("onesf")])
        self.op(self.DVE, lambda: nc.vector.memset(self.onesb[:], 1.0), W=[self.B("onesb")])

        self.op(self.DVE, lambda: nc.vector.memset(self.selb[:], 0.0), W=[self.B("selb")])
        self.op(self.DVE, lambda: nc.vector.memset(self.selb[:, 0:1], 1.0), W=[self.B("selb")])
        self.op(self.DVE, lambda: nc.vector.memset(self.selb[:, 3:4], 1.0), W=[self.B("selb")])

    def modulation(self):
        nc = self.nc
        with contextlib.ExitStack() as ph:
            for r in range(2):
                self.dma(self.SP, self.gT[:, :, r], self.norm_g[r].rearrange("(k p) -> p k", p=128), W=[self.B("gT")], slow=True)
            if self.kind == "A":
                scT = self.sb(ph, "scT", [128, 8, 2], F32)
                modrow = self.sb(ph, "modrow", [2, 6 * D], F32)
                brow = self.sb(ph, "brow", [2, 6 * D], F32)
                wb = [self.sb(ph, f"adaw{i}", [128, 8, 512], F32) for i in range(2)]
                for r in range(2):
                    self.dma(self.SP, scT[:, :, r], self.cvec[r].rearrange("(k p) -> p k", p=128), W=[self.B("scT")], slow=True)
                self.dma(self.SP, brow[:], self.ada_b.partition_broadcast(2), W=[self.B("brow")])
                self.op(self.ACT, lambda: nc.scalar.activation(out=scT[:], in_=scT[:], func=AF.Silu), R=[self.B("scT")], W=[self.B("scT")])
                wv = self.ada_w.rearrange("(k p) c -> p k c", p=128)
                for n in range(12):
                    w = wb[n % 2]
                    wbuf = self.B("adaw", n % 2)
                    for kk in range(0, 8, 4):
                        self.dma(self.SP, w[:, kk:kk + 4, :], wv[:, kk:kk + 4, n * 512:(n + 1) * 512], W=[wbuf])
                    pb, pbb = self.bank()

                    def mv(w=w, pb=pb):
                        for k in range(8):
                            last = self.mm(pb[0:2, :], scT[:, k, :], w[:, k, :], start=(k == 0), stop=(k == 7))
                        return last
                    self.op(self.PE, mv, R=[wbuf, self.B("scT")], W=[pbb])
                    self.op(self.DVE, lambda pb=pb, n=n: nc.vector.tensor_tensor(out=modrow[:, n * 512:(n + 1) * 512], in0=pb[0:2, :], in1=brow[:, n * 512:(n + 1) * 512], op=ALU.add),
                            R=[pbb, self.B("brow")], W=[self.B("modrow")])
                self.dma(self.SP, self.mod_d, modrow[:], R=[self.B("modrow")], W=[self.B("mod_d")])
                self.dma(self.SP, self.mod_o, modrow[:], R=[self.B("modrow")])
                if self.dbg:
                    self.dma(self.SP, self.dbg5, scT[:].rearrange("p k r -> p (k r)"), R=[self.B("scT")])
                    self.dma(self.SP, self.dbg6, modrow[:], R=[self.B("modrow")])

            for r in range(2):
                mv_ = self.mod_d[r].rearrange("(j p) -> p j", p=128)
                for j0 in range(0, 48, 16):
                    self.dma(self.SP, self.modT[:, j0:j0 + 16, r], mv_[:, j0:j0 + 16], R=[self.B("mod_d")], W=[self.B("modT")], slow=True)
            for ni in range(2):
                sh, sc = (0, 1) if ni == 0 else (3, 4)
                for r in range(2):
                    self.op(self.DVE, lambda ni=ni, sc=sc, r=r: nc.vector.scalar_tensor_tensor(
                        out=self.ab[:, 2 * ni, :, r], in0=self.modT[:, sc * 8:(sc + 1) * 8, r], scalar=1.0,
                        in1=self.gT[:, :, ni], op0=ALU.add, op1=ALU.mult),
                        R=[self.B("modT"), self.B("gT")], W=[self.B("ab")])
                self.op(self.DVE, lambda ni=ni, sh=sh: nc.vector.tensor_copy(self.ab[:, 2 * ni + 1, :, :], self.modT[:, sh * 8:(sh + 1) * 8, :]),
                        R=[self.B("modT")], W=[self.B("ab")])
            self.barrier()

    def norm_phase(self, ni):
        nc = self.nc
        with contextlib.ExitStack() as ph:
            junk = self.sb(ph, "junk", [128, D], BF16)
            xn = [self.sb(ph, f"xn{i}", [128, 4, D], BF16) for i in range(2)]
            for t in range(NT):
                self.op(self.ACT, lambda t=t: nc.scalar.activation(out=junk[:], in_=self.x[:, t, :], func=AF.Square, accum_out=self.ss[:, t:t + 1]),
                        R=[self.B("x", t)], W=[self.B("ss")])
            self.op(self.DVE, lambda: nc.vector.tensor_scalar(out=self.rstd[:], in0=self.ss[:], scalar1=1.0 / D, scalar2=EPS, op0=ALU.mult, op1=ALU.add),
                    R=[self.B("ss")], W=[self.B("rstd")])
            self.op(self.ACT, lambda: nc.scalar.activation(out=self.rstd[:], in_=self.rstd[:], func=AF.Sqrt), R=[self.B("rstd")], W=[self.B("rstd")])
            self.op(self.DVE, lambda: nc.vector.reciprocal(out=self.rstd[:], in_=self.rstd[:]), R=[self.B("rstd")], W=[self.B("rstd")])
            ev = 0
            for g, (c0, n) in enumerate(GROUPS):
                r = 0 if g < 4 else 1
                xg = xn[g % 2]
                xb = self.B("xn", g % 2)
                nt = n // 128
                for i in range(nt):
                    t = c0 // 128 + i
                    E = self.POOL if (i % 2) else self.DVE
                    ee = nc.gpsimd if (i % 2) else nc.vector
                    self.op(E, lambda ee=ee, i=i, t=t, xg=xg: ee.tensor_scalar(out=xg[:, i, :], in0=self.x[:, t, :], scalar1=self.rstd[:, t:t + 1], scalar2=None, op0=ALU.mult),
                            R=[self.B("x", t), self.B("rstd")], W=[xb])
                for k in range(8):
                    pb, pbb = self.bank()
                    pbf = pb[:].bitcast(BF16)

                    def tr(pbf=pbf, xg=xg, k=k, nt=nt):
                        for i in range(nt):
                            last = nc.tensor.transpose(pbf[:, i * 128:(i + 1) * 128], xg[:, i, k * 128:(k + 1) * 128], self.idb[:])
                        return last
                    self.op(self.PE, tr, R=[xb, self.B("idb")], W=[pbb])
                    a = self.ab[:, 2 * ni, k, r:r + 1]
                    b = self.ab[:, 2 * ni + 1, k, r:r + 1]
                    dst = self.hT[:, k, c0:c0 + n]
                    if ev % 2 == 0:
                        self.op(self.ACT, lambda pbf=pbf, a=a, b=b, dst=dst, n=n: nc.scalar.activation(out=dst, in_=pbf[:, 0:n], func=AF.Identity, scale=a, bias=b),
                                R=[pbb, self.B("ab")], W=[self.B("hT", g)])
                    else:
                        self.op(self.DVE, lambda pbf=pbf, a=a, b=b, dst=dst, n=n: nc.vector.tensor_scalar(out=dst, in0=pbf[:, 0:n], scalar1=a, scalar2=b, op0=ALU.mult, op1=ALU.add),
                                R=[pbb, self.B("ab")], W=[self.B("hT", g)])
                    ev += 1
            self.barrier()

    def load_w(self, wt, wbuf, src, c0, ncols):
        v = src.rearrange("(k p) c -> p k c", p=128)
        for kk in range(0, 8, 4):
            self.dma(self.POOL, wt[:, kk:kk + 4, 0:ncols], v[:, kk:kk + 4, c0:c0 + ncols], W=[wbuf])

    def proj_fm(self, ph, wpool, c0, ncols, rot_c0, dest, scale=1.0):
        nc = self.nc
        wt, wbuf = wpool()
        self.load_w(wt, wbuf, self.w_in, c0, ncols)
        if rot_c0 is not None:
            wr, wrbuf = wpool()
            self.load_w(wr, wrbuf, self.w_rot, rot_c0, ncols)
        for ci in range(ncols // 128):
            for g, (t0, n) in enumerate(GROUPS):
                pa, pab = self.bank()

                def acc(pa=pa, wt=wt, ci=ci, t0=t0, n=n):
                    for k in range(8):
                        last = self.mm(pa[:, 0:n], wt[:, k, ci * 128:(ci + 1) * 128], self.hT[:, k, t0:t0 + n], start=(k == 0), stop=(k == 7))
                    return last
                self.op(self.PE, acc, R=[wbuf, self.B("hT", g)], W=[pab])
                dst, dbufs, after = dest(ci, g, t0, n)
                if rot_c0 is not None:
                    pr, prb = self.bank()

                    def accr(pr=pr, wr=wr, ci=ci, t0=t0, n=n):
                        for k in range(8):
                            last = self.mm(pr[:, 0:n], wr[:, k, ci * 128:(ci + 1) * 128], self.hT[:, k, t0:t0 + n], start=(k == 0), stop=(k == 7))
                        return last
                    self.op(self.PE, accr, R=[wrbuf, self.B("hT", g)], W=[prb])
                    i = self.rt_i % 2
                    self.rt_i += 1
                    t1, t2 = self.rt1[i], self.rt2[i]
                    self.op(self.DVE, lambda pa=pa, t1=t1, t0=t0, n=n: nc.vector.tensor_tensor(out=t1[:, 0:n], in0=pa[:, 0:n], in1=self.cst[:, 0, t0:t0 + n], op=ALU.mult),
                            R=[pab, self.B("cst")], W=[self.B("rt1", i)])
                    self.op(self.DVE, lambda pr=pr, t2=t2, t0=t0, n=n: nc.vector.tensor_tensor(out=t2[:, 0:n], in0=pr[:, 0:n], in1=self.cst[:, 1, t0:t0 + n], op=ALU.mult),
                            R=[prb, self.B("cst")], W=[self.B("rt2", i)])
                    self.op(self.POOL, lambda t1=t1, t2=t2, dst=dst, n=n: nc.gpsimd.tensor_tensor(out=dst, in0=t1[:, 0:n], in1=t2[:, 0:n], op=ALU.add),
                            R=[self.B("rt1", i), self.B("rt2", i)], W=dbufs)
                else:
                    self.op(self.ACT, lambda pa=pa, dst=dst, n=n: nc.scalar.activation(out=dst, in_=pa[:, 0:n], func=AF.Copy, scale=scale),
                            R=[pab], W=dbufs)
                if after is not None:
                    after()

    def proj_tm(self, ph, wpool, c0, ncols, dst_dram, dcol0):
        nc = self.nc
        wt, wbuf = wpool()
        self.load_w(wt, wbuf, self.w_in, c0, ncols)
        for t in range(NT):
            g = min(t // 4, 4)
            pa, pab = self.bank()

            def acc(pa=pa, wt=wt, t=t):
                for k in range(8):
                    last = self.mm(pa[:, 0:ncols], self.hT[:, k, t * 128:(t + 1) * 128], wt[:, k, 0:ncols], start=(k == 0), stop=(k == 7))
                return last
            self.op(self.PE, acc, R=[wbuf, self.B("hT", g)], W=[pab])
            i = self.st_i % 2
            self.st_i += 1
            st = self.stg[i]
            self.op(self.ACT, lambda pa=pa, st=st: nc.scalar.activation(out=st[:, 0:ncols], in_=pa[:, 0:ncols], func=AF.Copy), R=[pab], W=[self.B("stg", i)])
            self.dma(self.SP, dst_dram[t * 128:(t + 1) * 128, dcol0:dcol0 + ncols], st[:, 0:ncols], R=[self.B("stg", i)], W=[self.B("vdram")])

    def stage_A_proj(self):
        nc = self.nc
        with contextlib.ExitStack() as ph:
            self.cst = self.sb(ph, "cst", [128, 2, TOK], F32)
            self.dma(self.SP, self.cst[:], self.cs, W=[self.B("cst")])
            wts = [self.sb(ph, f"wblk{i}", [128, 8, 512], BF16) for i in range(3)]
            self.w_i = 0

            def wpool():
                i = self.w_i % 3
                self.w_i += 1
                return wts[i], self.B("wblk", i)
            self.rt1 = [self.sb(ph, f"rt1{i}", [128, 512], F32) for i in range(2)]
            self.rt2 = [self.sb(ph, f"rt2{i}", [128, 512], F32) for i in range(2)]
            self.rt_i = 0
            self.stg = [self.sb(ph, f"stg{i}", [128, 512], BF16) for i in range(2)]
            self.st_i = 0

            def kdest(row0):
                def dest(ci, g, t0, n):
                    i = self.st_i % 2
                    self.st_i += 1
                    st = self.stg[i]

                    def after(st=st, i=i, ci=ci, t0=t0, n=n):
                        self.dma(self.SP, self.kT_o[row0 + ci * 128:row0 + (ci + 1) * 128, t0:t0 + n], st[:, 0:n], R=[self.B("stg", i)], W=[self.B("kdram")])
                    return st[:, 0:n], [self.B("stg", i)], after
                return dest
            if self.even:
                self.proj_fm(ph, wpool, 1024, 512, 512, kdest(0))
                self.proj_tm(ph, wpool, 1536, 512, self.v_o, 0)
                self.proj_fm(ph, wpool, 2048, 512, None, kdest(512))
                self.proj_tm(ph, wpool, 2560, 512, self.v_o, 512)
            else:
                self.proj_fm(ph, wpool, 1024, 256, 1024, kdest(0))
                self.proj_tm(ph, wpool, 1280, 256, self.v_o, 0)
            self.barrier()


    def stage_B_qproj(self, ph):
        nc = self.nc
        with contextlib.ExitStack() as p2:
            self.cst = self.sb(p2, "cst", [128, 2, TOK], F32)
            self.dma(self.SP, self.cst[:], self.cs, W=[self.B("cst")])
            wts = [self.sb(p2, f"wblk{i}", [128, 8, 512], BF16) for i in range(3)]
            self.w_i = 0

            def wpool():
                i = self.w_i % 3
                self.w_i += 1
                return wts[i], self.B("wblk", i)
            self.rt1 = [self.sb(p2, f"rt1{i}", [128, 512], F32) for i in range(2)]
            self.rt2 = [self.sb(p2, f"rt2{i}", [128, 512], F32) for i in range(2)]
            self.rt_i = 0

            def qdest(ch0):
                def dest(ci, g, t0, n):
                    return self.QT[:, ch0 + ci, t0:t0 + n], [self.B("QT", ch0 + ci, g)], None
                return dest
            if self.even:
                self.proj_fm(p2, wpool, 0, 512, 0, qdest(0))
                self.proj_fm(p2, wpool, 512, 512, None, qdest(4), scale=0.125)
            else:
                self.proj_fm(p2, wpool, 0, 512, 0, qdest(0))
                self.proj_fm(p2, wpool, 512, 512, 512, qdest(4))
            self.barrier()

    def small_attn(self, ph, slots, qn, nkt, kT, v, bias, scale, dests, qbufs, kvbufs, sinkrow=None):
        nc = self.nc
        import os
        LVL = int(os.environ.get("K_ATTN_LEVEL", "9"))
        if LVL == 0:
            return
        ns = len(slots)
        W = ns * qn
        nu = len(dests)
        accs = [self.fbank(4 + u) for u in range(nu)]
        accl, acclb = self.fbank(6)
        pend = []

        nh = ns // 2
        h_slots = [[si for si, sl in enumerate(slots) if sl[0] == h] for h in range(2)]
        colmap = {}
        for h in range(2):
            for k, si in enumerate(h_slots[h]):
                colmap[si] = (h * nh + k) * qn
        col = lambda si: colmap[si]

        def s_tile(kt):
            pss = [self.bank(), self.bank()]

            def f():
                for h in range(2):
                    ps = pss[h][0]
                    for k, si in enumerate(h_slots[h]):
                        q = slots[si][2]
                        b = None if os.environ.get("K_NOBIAS") else bias(si, kt)
                        last = self.mm(ps[:, k * qn:(k + 1) * qn], kT(si, kt), q, start=True, stop=(b is None))
                        if b is not None:
                            last = self.mm(ps[:, k * qn:(k + 1) * qn], self.idb[:], b, start=False, stop=True)
                return last
            self.op(self.PE, f, R=qbufs + kvbufs + [self.B("idb")], W=[pss[0][1], pss[1][1]])
            i = self.pt_i % 3
            self.pt_i += 1
            pt = self.pts[i]
            for h in range(2):
                self.op(self.ACT, lambda h=h: nc.scalar.activation(out=pt[:, h * nh * qn:(h + 1) * nh * qn], in_=pss[h][0][:, 0:nh * qn], func=AF.Exp, scale=scale),
                        R=[pss[h][1]], W=[self.B("pt", i)])
            return pt, i

        def pv_tile(kt, pt, i):
            if LVL < 2:
                return

            def f():
                for si, (half, u, q) in enumerate(slots):
                    self.mm(accs[u][0][half * 64:(half + 1) * 64, 0:qn], v(si, kt), pt[:, col(si):col(si) + qn], start=(kt == 0), stop=(kt == nkt - 1))
                return self.mm(accl[0:1, 0:W], self.onesb[:, 0:1], pt[:, 0:W], start=(kt == 0), stop=(kt == nkt - 1 and sinkrow is None))
            self.op(self.PE, f, R=[self.B("pt", i)] + kvbufs, W=[a[1] for a in accs] + [acclb])
        for kt in range(nkt):
            pend.append((kt,) + s_tile(kt))
            if len(pend) > 1:
                pv_tile(*pend.pop(0))
        while pend:
            pv_tile(*pend.pop(0))
        if LVL < 3:
            return
        if sinkrow is not None:
            def fsink():
                self.mm(accl[0:1, 0:W], self.onesb[0:1, 0:1], sinkrow[0], start=False, stop=False)
                return self.mm(accl[0:1, 0:W], self.onesb[0:1, 0:1], sinkrow[1], start=False, stop=True)
            self.op(self.PE, fsink, R=[self.B("sinkrow")], W=[acclb])
        self.op(self.DVE, lambda: nc.vector.reciprocal(out=self.rl[0:1, 0:W], in_=accl[0:1, 0:W]), R=[acclb], W=[self.B("rl")])
        self.split_hl(self.rl[0:1, 0:W], self.rlh[0:1, 0:W], self.rll[0:1, 0:W], self.B("rl"), self.B("rlh"), self.B("rll"))
        bc, bcb = self.fbank(7)

        def fbc():
            for si, (half, u, q) in enumerate(slots):
                self.mm(bc[half * 64:(half + 1) * 64, u * qn:(u + 1) * qn], self.onesb[0:1, 0:64], self.rlh[0:1, col(si):col(si) + qn], start=True, stop=False)
                last = self.mm(bc[half * 64:(half + 1) * 64, u * qn:(u + 1) * qn], self.onesb[0:1, 0:64], self.rll[0:1, col(si):col(si) + qn], start=False, stop=True)
            return last
        self.op(self.PE, fbc, R=[self.B("rlh"), self.B("rll"), self.B("onesb")], W=[bcb])
        self.op(self.ACT, lambda: nc.scalar.activation(out=self.bcs[:, 0:nu * qn], in_=bc[:, 0:nu * qn], func=AF.Copy), R=[bcb], W=[self.B("bcs")])
        for u, (dst, dbufs) in enumerate(dests):
            self.op(self.DVE, lambda u=u, dst=dst: nc.vector.tensor_tensor(out=dst, in0=accs[u][0][:, 0:qn], in1=self.bcs[:, u * qn:(u + 1) * qn], op=ALU.mult),
                    R=[accs[u][1], self.B("bcs")], W=dbufs)

    def split_hl(self, src, hi, lo, sb, hb, lb):
        nc = self.nc
        self.op(self.DVE, lambda: nc.vector.tensor_copy(hi, src), R=[sb], W=[hb])
        self.op(self.DVE, lambda: nc.vector.tensor_tensor(out=lo, in0=src, in1=hi, op=ALU.subtract), R=[sb, hb], W=[lb])

    def attn_scratch(self, ph):
        self.pts = [self.sb(ph, f"pt{i}", [128, 512], BF16) for i in range(4)]
        self.pt_i = 0
        self.rl = self.sb(ph, "rl", [2, 512], F32)
        self.rlh = self.sb(ph, "rlh", [2, 512], BF16)
        self.rll = self.sb(ph, "rll", [2, 512], BF16)
        self.bcs = self.sb(ph, "bcs", [128, 512], F32)

    def na_phase(self, ph):
        nc = self.nc
        with contextlib.ExitStack() as p:
            wk = self.sb(p, "wk", [128, WIN * 128], BF16)
            wv = self.sb(p, "wv", [128, WIN, 128], BF16)
            ck = self.sb(p, "ck", [128, 256], BF16)
            cv = self.sb(p, "cv", [128, 2, 128], BF16)
            nbs = [self.sb(p, f"nb{i}", [128, 2, 7, 128], BF16) for i in range(2)]
            nb_i = 0
            import os
            _cs = [4] if os.environ.get("K_SMALL") else range(4, 8)
            _ts = [0, 1] if os.environ.get("K_SMALL") else range(16)
            for c in _cs:
                hc = c - 4
                kvb = [self.B("nakv")]
                self.dma(self.SP, wk[:], self.win_kT[hc * 128:(hc + 1) * 128, :], W=kvb)
                for w0 in (0, 11):
                    self.dma(self.SP, wv[:, w0:w0 + 11, :], self.win_v.rearrange("(t p) c -> p t c", p=128)[:, w0:w0 + 11, hc * 128:(hc + 1) * 128], W=kvb)
                self.dma(self.SP, ck[:], self.nck[hc * 128:(hc + 1) * 128, :], W=kvb)
                self.dma(self.SP, cv[:], self.ncv.rearrange("(t p) c -> p t c", p=128)[:, :, hc * 128:(hc + 1) * 128], W=kvb)
                for t in _ts:
                    slot = {0: 0, 1: 1, 14: 3, 15: 4}.get(t, 2)
                    nb = nbs[nb_i % 2]
                    nbb = self.B("nb", nb_i % 2)
                    nb_i += 1
                    self.dma(self.POOL, nb[:].rearrange("p h j q -> p (h j q)"), self.nab[slot, :, (2 * hc) * 896:(2 * hc + 2) * 896], W=[nbb])
                    slots = [(0, 0, self.QT[0:64, c, t * 128:(t + 1) * 128]), (1, 0, self.QT[64:128, c, t * 128:(t + 1) * 128])]

                    def kT(si, kt, t=t):
                        lo = si * 64
                        if kt < 2:
                            return ck[lo:lo + 64, kt * 128:(kt + 1) * 128]
                        w = t + kt - 2
                        return wk[lo:lo + 64, w * 128:(w + 1) * 128]

                    def vv(si, kt, t=t):
                        if kt < 2:
                            return cv[:, kt, si * 64:(si + 1) * 64]
                        return wv[:, t + kt - 2, si * 64:(si + 1) * 64]

                    def bias(si, kt, nb=nb):
                        if kt < 2:
                            return None
                        return nb[:, si, kt - 2, :]
                    g = t // 4
                    self.small_attn(p, slots, 128, 9, kT, vv, bias, 1.0, [(self.hT[:, c, t * 128:(t + 1) * 128], [self.B("hT", g)])],
                                    [self.B("QT", c, g)], kvb + [nbb])
                slots = [(0, 0, self.QT[0:64, c, OWN:TOK]), (1, 0, self.QT[64:128, c, OWN:TOK])]
                self.small_attn(p, slots, 256, 2, lambda si, kt: ck[si * 64:si * 64 + 64, kt * 128:(kt + 1) * 128],
                                lambda si, kt: cv[:, kt, si * 64:(si + 1) * 64], lambda si, kt: None, 1.0,
                                [(self.hT[:, c, OWN:TOK], [self.B("hT", 4)])], [self.B("QT", c, 4)], kvb)
            self.barrier()

    def swa_phase(self, ph):
        nc = self.nc
        with contextlib.ExitStack() as p:
            wk = self.sb(p, "wk", [128, WIN * 128], BF16)
            wv = self.sb(p, "wv", [128, WIN, 64], BF16)
            ck = self.sb(p, "ck", [128, 256], BF16)
            cv = self.sb(p, "cv", [128, 2, 64], BF16)
            msk = self.sb(p, "msk", [128, 4, 128], BF16)
            es = self.sb(p, "es", [1, 16], F32)
            zr = self.sb(p, "zr", [1, 128], F32)
            srow = self.sb(p, "srow", [1, 4, 512], F32)
            srh = self.sb(p, "srh", [1, 4, 512], BF16)
            srl = self.sb(p, "srl", [1, 4, 512], BF16)
            self.dma(self.POOL, msk[:].rearrange("p s q -> p (s q)"), self.swam, W=[self.B("msk")])
            self.dma(self.SP, es[:], self.sinks, W=[self.B("es")])
            self.op(self.ACT, lambda: nc.scalar.activation(out=es[:], in_=es[:], func=AF.Exp), R=[self.B("es")], W=[self.B("es")])
            self.op(self.DVE, lambda: nc.vector.memset(zr[:], 0.0), W=[self.B("zr")])
            for j in range(4):
                for pos, si in enumerate((0, 2, 1, 3)):
                    h = 4 * j + si
                    self.op(self.DVE, lambda j=j, si=pos, h=h: nc.vector.tensor_scalar(out=srow[0:1, j, si * 128:(si + 1) * 128], in0=zr[0:1, :], scalar1=es[0:1, h:h + 1], scalar2=None, op0=ALU.add),
                            R=[self.B("zr"), self.B("es")], W=[self.B("sinkrow")])
            self.split_hl(srow[0:1, :, :], srh[0:1, :, :], srl[0:1, :, :], self.B("sinkrow"), self.B("srh"), self.B("sinkrow"))
            for j in range(4):
                kvb = [self.B("swkv")]
                for half in range(2):
                    self.dma(self.SP, wk[half * 64:(half + 1) * 64, :], self.win_kT[j * 64:(j + 1) * 64, :], W=kvb)
                    self.dma(self.SP, ck[half * 64:(half + 1) * 64, :], self.kT_g[0, j * 64:(j + 1) * 64, OWN:TOK], W=kvb)
                for w0 in (0, 11):
                    self.dma(self.SP, wv[:, w0:w0 + 11, :], self.win_v.rearrange("(t p) c -> p t c", p=128)[:, w0:w0 + 11, j * 64:(j + 1) * 64], W=kvb)
                self.dma(self.SP, cv[:], self.v_g[0, OWN:TOK, :].rearrange("(t p) c -> p t c", p=128)[:, :, j * 64:(j + 1) * 64], W=kvb)
                for t in range(NT):
                    own = t < 16
                    slots = []
                    for si in range(4):
                        ch, half = 2 * j + si // 2, si % 2
                        slots.append((half, si // 2, self.QT[half * 64:(half + 1) * 64, ch, t * 128:(t + 1) * 128]))

                    def kT(si, kt, t=t, own=own):
                        lo = (si % 2) * 64
                        if kt < 2:
                            return ck[lo:lo + 64, kt * 128:(kt + 1) * 128]
                        w = t + kt
                        return wk[lo:lo + 64, w * 128:(w + 1) * 128]

                    def vv(si, kt, t=t):
                        if kt < 2:
                            return cv[:, kt, :]
                        return wv[:, t + kt, :]

                    def bias(si, kt, t=t):
                        if kt == 2:
                            return msk[:, 2 if t == 0 else 0, :]
                        if kt == 4:
                            return msk[:, 3 if t == 15 else 1, :]
                        return None
                    g = min(t // 4, 4)
                    dests = [(self.hT[:, 2 * j + u, t * 128:(t + 1) * 128], [self.B("hT", g)]) for u in range(2)]
                    self.small_attn(p, slots, 128, 5 if own else 2, kT, vv, bias, 0.125, dests,
                                    [self.B("QT", 2 * j, g), self.B("QT", 2 * j + 1, g)], kvb + [self.B("msk")], sinkrow=(srh[0:1, j, :], srl[0:1, j, :]))
            self.barrier()

    def diff_phase(self, ph):
        nc = self.nc
        lam_d = nc.dram_tensor("lam_d", [1, 1], F32).ap()
        with contextlib.ExitStack() as p:
            kbs = [self.sb(p, f"kb{i}", [128, 2048], BF16) for i in range(2)]
            vbs = [self.sb(p, f"vb{i}", [128, 16, 128], BF16) for i in range(2)]
            lrow = self.sb(p, "lrow", [1, 256], F32)
            lt = self.sb(p, "lt", [1, 128], F32)
            l2 = self.sb(p, "l2", [1, 4], F32)
            lc = self.sb(p, "lc", [128, 2], F32)
            sg = self.sb(p, "sg", [128, 1], F32)
            nlam = self.sb(p, "nlam", [128, 1], F32)
            bcs1 = self.sb(p, "bcs1", [128, 512], F32)
            t0s = self.sb(p, "t0s", [128, 512], F32)
            t1s = self.sb(p, "t1s", [128, 512], F32)
            sqh = self.sb(p, "sqh", [128, 512], BF16)
            sql = self.sb(p, "sql", [128, 512], BF16)
            self.dma(self.SP, lrow[:], self.lam, W=[self.B("lrow")])
            self.dma(self.SP, lc[:], self.lamc, W=[self.B("lc")])
            self.dma(self.SP, sg[:], self.subg, W=[self.B("sg")])
            self.op(self.DVE, lambda: nc.vector.tensor_tensor(out=lt[0:1, 0:64], in0=lrow[0:1, 0:64], in1=lrow[0:1, 64:128], op=ALU.mult), R=[self.B("lrow")], W=[self.B("lt")])
            self.op(self.DVE, lambda: nc.vector.tensor_tensor(out=lt[0:1, 64:128], in0=lrow[0:1, 128:192], in1=lrow[0:1, 192:256], op=ALU.mult), R=[self.B("lrow")], W=[self.B("lt")])
            self.op(self.DVE, lambda: nc.vector.reduce_sum(out=l2[0:1, 0:1], in_=lt[0:1, 0:64], axis=AX.X), R=[self.B("lt")], W=[self.B("l2")])
            self.op(self.DVE, lambda: nc.vector.reduce_sum(out=l2[0:1, 1:2], in_=lt[0:1, 64:128], axis=AX.X), R=[self.B("lt")], W=[self.B("l2")])
            self.op(self.ACT, lambda: nc.scalar.activation(out=l2[0:1, 0:2], in_=l2[0:1, 0:2], func=AF.Exp), R=[self.B("l2")], W=[self.B("l2")])
            self.op(self.DVE, lambda: nc.vector.tensor_tensor(out=l2[0:1, 2:3], in0=l2[0:1, 1:2], in1=l2[0:1, 0:1], op=ALU.subtract), R=[self.B("l2")], W=[self.B("l2")])
            self.op(self.DVE, lambda: nc.vector.tensor_tensor(out=l2[0:1, 3:4], in0=l2[0:1, 2:3], in1=lc[0:1, 0:1], op=ALU.subtract), R=[self.B("l2"), self.B("lc")], W=[self.B("l2")])
            self.dma(self.SP, lam_d, l2[0:1, 3:4], R=[self.B("l2")], W=[self.B("lam_d")])
            self.dma(self.SP, nlam[:], lam_d[0, 0:1].partition_broadcast(128), R=[self.B("lam_d")], W=[self.B("nlam")])
            self.op(self.DVE, lambda: nc.vector.tensor_tensor(out=sg[:], in0=sg[:], in1=lc[:, 1:2], op=ALU.mult), R=[self.B("sg"), self.B("lc")], W=[self.B("sg")])
            kv_i = 0
            _dhs = [0] if os.environ.get("K_SMALLD") else range(4)
            for dh in _dhs:
                for g, (q0, qn) in enumerate(GROUPS):
                    if os.environ.get("K_SMALLD") and g not in (0, 4):
                        continue
                    blocks = [(0, OWN, 256)]
                    if g < 4:
                        blocks += [(rb, 0, 2048) for rb in range(NC)]
                    acc0, a0b = self.fbank(4)
                    acc1, a1b = self.fbank(5)
                    accl, alb = self.fbank(6)
                    nblk = len(blocks)
                    ntl = sum(nk // 128 for (_, _, nk) in blocks)
                    qb = [self.B("QT", dh, g)]
                    pend = []
                    kvbase = kv_i
                    kv_i += nblk

                    def issue(bi):
                        rb, k0, nk = blocks[bi]
                        i = (kvbase + bi) % 2
                        kb, vb = kbs[i], vbs[i]
                        kvbuf = self.B("dkv", i)
                        self.dma(self.SP, kb[:, 0:nk], self.kT_g[rb, dh * 128:(dh + 1) * 128, k0:k0 + nk], W=[kvbuf])
                        vsrc = self.v_g[rb, k0:k0 + nk, :].rearrange("(t p) c -> p t c", p=128)
                        for w0 in range(0, nk // 128, 8):
                            w1 = min(w0 + 8, nk // 128)
                            self.dma(self.SP, vb[:, w0:w1, :], vsrc[:, w0:w1, dh * 128:(dh + 1) * 128], W=[kvbuf])
                        return kb, vb, kvbuf

                    def s_tile(idx, kb, vb, kvbuf, kt):
                        ps0, p0b = self.bank()
                        ps1, p1b = self.bank()

                        def f():
                            self.mm(ps0[:, 0:qn], kb[0:64, kt * 128:(kt + 1) * 128], self.QT[0:64, dh, q0:q0 + qn])
                            return self.mm(ps1[:, 0:qn], kb[64:128, kt * 128:(kt + 1) * 128], self.QT[64:128, dh, q0:q0 + qn])
                        self.op(self.PE, f, R=qb + [kvbuf], W=[p0b, p1b])
                        i0 = self.pt_i % 4
                        i1 = (self.pt_i + 1) % 4
                        self.pt_i += 2
                        self.op(self.ACT, lambda: nc.scalar.activation(out=self.pts[i0][:, 0:qn], in_=ps0[:, 0:qn], func=AF.Exp, scale=0.125), R=[p0b], W=[self.B("pt", i0)])
                        self.op(self.ACT, lambda: nc.scalar.activation(out=self.pts[i1][:, 0:qn], in_=ps1[:, 0:qn], func=AF.Exp, scale=0.125), R=[p1b], W=[self.B("pt", i1)])
                        return idx, vb, kvbuf, kt, i0, i1

                    def pv_tile(idx, vb, kvbuf, kt, i0, i1):
                        first, last = idx == 0, idx == ntl - 1

                        def f():
                            self.mm(acc0[:, 0:qn], vb[:, kt, :], self.pts[i0][:, 0:qn], start=first, stop=last)
                            self.mm(acc1[:, 0:qn], vb[:, kt, :], self.pts[i1][:, 0:qn], start=first, stop=last)
                            self.mm(accl[0:2, 0:qn], self.selb[:, 0:2], self.pts[i0][:, 0:qn], start=first, stop=False)
                            return self.mm(accl[0:2, 0:qn], self.selb[:, 2:4], self.pts[i1][:, 0:qn], start=False, stop=last)
                        self.op(self.PE, f, R=[self.B("pt", i0), self.B("pt", i1), kvbuf, self.B("selb")], W=[a0b, a1b, alb])
                    cur = issue(0)
                    idx = 0
                    for bi in range(nblk):
                        kb, vb, kvbuf = cur
                        nxt = None
                        for kt in range(blocks[bi][2] // 128):
                            pend.append(s_tile(idx, kb, vb, kvbuf, kt))
                            idx += 1
                            if len(pend) > 1:
                                pv_tile(*pend.pop(0))
                            if kt == 0 and bi + 1 < nblk:
                                nxt = issue(bi + 1)
                        cur = nxt
                    while pend:
                        pv_tile(*pend.pop(0))
                    self.op(self.DVE, lambda: nc.vector.reciprocal(out=self.rl[0:2, 0:qn], in_=accl[0:2, 0:qn]), R=[alb], W=[self.B("rl")])
                    bc, bcb = self.fbank(7)
                    self.split_hl(self.rl[0:2, 0:qn], self.rlh[0:2, 0:qn], self.rll[0:2, 0:qn], self.B("rl"), self.B("rlh"), self.B("rll"))

                    def fb(c0):
                        self.mm(bc[:, 0:qn], self.e2b[0:2, c0:c0 + 128], self.rlh[0:2, 0:qn], start=True, stop=False)
                        return self.mm(bc[:, 0:qn], self.e2b[0:2, c0:c0 + 128], self.rll[0:2, 0:qn], start=False, stop=True)
                    self.op(self.PE, lambda: fb(0), R=[self.B("rlh"), self.B("rll"), self.B("e2b")], W=[bcb])
                    self.op(self.ACT, lambda: nc.scalar.activation(out=self.bcs[:, 0:qn], in_=bc[:, 0:qn], func=AF.Copy), R=[bcb], W=[self.B("bcs")])
                    self.op(self.PE, lambda: fb(128), R=[self.B("rlh"), self.B("rll"), self.B("e2b"), self.B("bcs")], W=[bcb])
                    self.op(self.ACT, lambda: nc.scalar.activation(out=bcs1[:, 0:qn], in_=bc[:, 0:qn], func=AF.Copy, scale=nlam[:, 0:1]), R=[bcb, self.B("nlam")], W=[self.B("bcs1")])
                    self.op(self.DVE, lambda: nc.vector.tensor_tensor(out=t0s[:, 0:qn], in0=acc0[:, 0:qn], in1=self.bcs[:, 0:qn], op=ALU.mult), R=[a0b, self.B("bcs")], W=[self.B("t0s")])
                    self.op(self.DVE, lambda: nc.vector.tensor_tensor(out=t1s[:, 0:qn], in0=acc1[:, 0:qn], in1=bcs1[:, 0:qn], op=ALU.mult), R=[a1b, self.B("bcs1")], W=[self.B("t1s")])
                    self.op(self.DVE, lambda: nc.vector.tensor_tensor(out=t0s[:, 0:qn], in0=t0s[:, 0:qn], in1=t1s[:, 0:qn], op=ALU.add), R=[self.B("t1s")], W=[self.B("t0s")])
                    self.op(self.POOL, lambda: nc.gpsimd.tensor_tensor(out=t1s[:, 0:qn], in0=t0s[:, 0:qn], in1=t0s[:, 0:qn], op=ALU.mult), R=[self.B("t0s")], W=[self.B("t1s")])
                    self.split_hl(t1s[:, 0:qn], sqh[:, 0:qn], sql[:, 0:qn], self.B("t1s"), self.B("sqh"), self.B("sql"))

                    def fss():
                        self.mm(bc[:, 0:qn], self.onesb[:], sqh[:, 0:qn], start=True, stop=False)
                        return self.mm(bc[:, 0:qn], self.onesb[:], sql[:, 0:qn], start=False, stop=True)
                    self.op(self.PE, fss, R=[self.B("sqh"), self.B("sql"), self.B("onesb"), self.B("bcs1")], W=[bcb])
                    self.op(self.DVE, lambda: nc.vector.tensor_scalar(out=self.bcs[:, 0:qn], in0=bc[:, 0:qn], scalar1=1.0 / 128, scalar2=EPS, op0=ALU.mult, op1=ALU.add), R=[bcb], W=[self.B("bcs")])
                    self.op(self.ACT, lambda: nc.scalar.activation(out=self.bcs[:, 0:qn], in_=self.bcs[:, 0:qn], func=AF.Sqrt), R=[self.B("bcs")], W=[self.B("bcs")])
                    self.op(self.DVE, lambda: nc.vector.reciprocal(out=self.bcs[:, 0:qn], in_=self.bcs[:, 0:qn]), R=[self.B("bcs")], W=[self.B("bcs")])
                    self.op(self.DVE, lambda: nc.vector.scalar_tensor_tensor(out=self.hT[:, dh, q0:q0 + qn], in0=t0s[:, 0:qn], scalar=sg[:, 0:1], in1=self.bcs[:, 0:qn], op0=ALU.mult, op1=ALU.mult),
                            R=[self.B("t0s"), self.B("bcs"), self.B("sg")], W=[self.B("hT", g)])
            self.barrier()

    def load_gates(self, p, which):
        gts = []
        for r in range(2):
            gt = self.sb(p, f"gate{r}", [128, D], F32)
            self.dma(self.SP, gt[:], self.mod_d[r, which * D:(which + 1) * D].partition_broadcast(128), W=[self.B("gate", r)])
            gts.append(gt)
        return gts

    def outproj_phase(self):
        nc = self.nc
        with contextlib.ExitStack() as p:
            wo = self.sb(p, "wo", [128, 8, D], BF16)
            gts = self.load_gates(p, 2)
            tmps = [self.sb(p, f"tmp{i}", [128, 512], F32) for i in range(2)]
            wv = self.w_out.rearrange("(k p) c -> p k c", p=128)
            for kk in range(0, 8, 2):
                self.dma(self.POOL, wo[:, kk:kk + 2, :], wv[:, kk:kk + 2, :], W=[self.B("wo")])
            ctr = 0
            for t in range(NT):
                g = min(t // 4, 4)
                r = 0 if t < 16 else 1
                for half in range(2):
                    po, pob = self.fbank(4 + ctr % 2)

                    def f(po=po, t=t, half=half):
                        for k in range(8):
                            last = self.mm(po[:, :], self.hT[:, k, t * 128:(t + 1) * 128], wo[:, k, half * 512:(half + 1) * 512], start=(k == 0), stop=(k == 7))
                        return last
                    self.op(self.PE, f, R=[self.B("hT", g), self.B("wo")], W=[pob])
                    tmp = tmps[ctr % 2]
                    tb = self.B("tmp", ctr % 2)
                    self.op(self.DVE, lambda po=po, tmp=tmp, r=r, half=half: nc.vector.tensor_tensor(out=tmp[:], in0=po[:, :], in1=gts[r][:, half * 512:(half + 1) * 512], op=ALU.mult),
                            R=[pob, self.B("gate", r)], W=[tb])
                    self.op(self.POOL, lambda tmp=tmp, t=t, half=half: nc.gpsimd.tensor_tensor(out=self.x[:, t, half * 512:(half + 1) * 512], in0=self.x[:, t, half * 512:(half + 1) * 512], in1=tmp[:], op=ALU.add),
                            R=[tb], W=[self.B("x", t)])
                    ctr += 1
            self.barrier()

    def ffn_phase(self):
        nc = self.nc
        with contextlib.ExitStack() as p:
            gts = self.load_gates(p, 5)
            w1s = [self.sb(p, f"w1b{i}", [128, 8, 512], BF16) for i in range(2)]
            w3s = [self.sb(p, f"w3b{i}", [128, 8, 512], BF16) for i in range(2)]
            w2s = [self.sb(p, f"w2b{i}", [128, 4, D], BF16) for i in range(2)]
            aTs = [self.sb(p, f"aT{i}", [128, 4, 512], BF16) for i in range(2)]
            sgs = [self.sb(p, f"sgt{i}", [128, 512], F32) for i in range(2)]
            tmps = [self.sb(p, f"tmp{i}", [128, 512], F32) for i in range(2)]
            v1 = self.w1.rearrange("(k p) c -> p k c", p=128)
            v3 = self.w3.rearrange("(k p) c -> p k c", p=128)
            v2 = self.w2.rearrange("(c p) d -> p c d", p=128)
            ctr = 0
            ai = 0
            si = 0
            nblk = (FFN + 511) // 512
            for jb in range(nblk):
                ncol = min(512, FFN - jb * 512)
                nch = ncol // 128
                w1b, w3b, w2b = w1s[jb % 2], w3s[jb % 2], w2s[jb % 2]
                wb = self.B("ffw", jb % 2)
                for kk in range(0, 8, 4):
                    self.dma(self.POOL, w1b[:, kk:kk + 4, 0:ncol], v1[:, kk:kk + 4, jb * 512:jb * 512 + ncol], W=[wb])
                    self.dma(self.POOL, w3b[:, kk:kk + 4, 0:ncol], v3[:, kk:kk + 4, jb * 512:jb * 512 + ncol], W=[wb])
                self.dma(self.POOL, w2b[:, 0:nch, :], v2[:, jb * 4:jb * 4 + nch, :], W=[wb])
                for g, (t0, n) in enumerate(GROUPS):
                    r = 0 if g < 4 else 1
                    aT = aTs[ai % 2]
                    ab_ = self.B("aT", ai % 2)
                    ai += 1
                    for c in range(nch):
                        pu, pub = self.bank()
                        pg, pgb = self.bank()

                        def fu(pu=pu, w=w1b, c=c, t0=t0, n=n):
                            for k in range(8):
                                last = self.mm(pu[:, 0:n], w[:, k, c * 128:(c + 1) * 128], self.hT[:, k, t0:t0 + n], start=(k == 0), stop=(k == 7))
                            return last
                        self.op(self.PE, fu, R=[wb, self.B("hT", g)], W=[pub])
                        self.op(self.PE, lambda pg=pg, c=c, t0=t0, n=n: fu(pg, w3b, c, t0, n), R=[wb, self.B("hT", g)], W=[pgb])
                        sgt = sgs[si % 2]
                        sb_ = self.B("sgt", si % 2)
                        si += 1
                        self.op(self.ACT, lambda pu=pu, sgt=sgt, n=n: nc.scalar.activation(out=sgt[:, 0:n], in_=pu[:, 0:n], func=AF.Silu), R=[pub], W=[sb_])
                        self.op(self.DVE, lambda pg=pg, sgt=sgt, aT=aT, c=c, n=n: nc.vector.tensor_tensor(out=aT[:, c, 0:n], in0=pg[:, 0:n], in1=sgt[:, 0:n], op=ALU.mult),
                                R=[pgb, sb_], W=[ab_])
                    for i in range(n // 128):
                        t = t0 // 128 + i
                        for half in range(2):
                            po, pob = self.fbank(4 + ctr % 2)

                            def f(po=po, aT=aT, i=i, half=half):
                                for c in range(nch):
                                    last = self.mm(po[:, :], aT[:, c, i * 128:(i + 1) * 128], w2b[:, c, half * 512:(half + 1) * 512], start=(c == 0), stop=(c == nch - 1))
                                return last
                            self.op(self.PE, f, R=[ab_, wb], W=[pob])
                            tmp = tmps[ctr % 2]
                            tb = self.B("tmp", ctr % 2)
                            self.op(self.DVE, lambda po=po, tmp=tmp, r=r, half=half: nc.vector.tensor_tensor(out=tmp[:], in0=po[:, :], in1=gts[r][:, half * 512:(half + 1) * 512], op=ALU.mult),
                                    R=[pob, self.B("gate", r)], W=[tb])
                            self.op(self.POOL, lambda tmp=tmp, t=t, half=half: nc.gpsimd.tensor_tensor(out=self.x[:, t, half * 512:(half + 1) * 512], in0=self.x[:, t, half * 512:(half + 1) * 512], in1=tmp[:], op=ALU.add),
                                    R=[tb], W=[self.B("x", t)])
                            ctr += 1
            self.barrier()

    def output_phase(self):
        nc = self.nc
        with contextlib.ExitStack() as p:
            xo = self.xout.rearrange("(t p) d -> p t d", p=128)
            for t0 in range(0, NT, 3):
                self.dma(self.SP, xo[:, t0:t0 + 3, :], self.x[:, t0:t0 + 3, :], R=[self.B("x", t) for t in range(t0, t0 + 3)])
            junk = self.sb(p, "junk", [128, D], BF16)
            fgb = self.sb(p, "fgb", [128, D], F32)
            ys = [self.sb(p, f"ys{i}", [128, D], F32) for i in range(2)]
            self.dma(self.SP, fgb[:], self.final_g.partition_broadcast(128), W=[self.B("fgb")])
            for t in range(16):
                self.op(self.ACT, lambda t=t: nc.scalar.activation(out=junk[:], in_=self.x[:, t, :], func=AF.Square, accum_out=self.ss[:, t:t + 1]),
                        R=[self.B("x", t)], W=[self.B("ss")])
            self.op(self.DVE, lambda: nc.vector.tensor_scalar(out=self.rstd[:, 0:16], in0=self.ss[:, 0:16], scalar1=1.0 / D, scalar2=EPS, op0=ALU.mult, op1=ALU.add),
                    R=[self.B("ss")], W=[self.B("rstd")])
            self.op(self.ACT, lambda: nc.scalar.activation(out=self.rstd[:, 0:16], in_=self.rstd[:, 0:16], func=AF.Sqrt), R=[self.B("rstd")], W=[self.B("rstd")])
            self.op(self.DVE, lambda: nc.vector.reciprocal(out=self.rstd[:, 0:16], in_=self.rstd[:, 0:16]), R=[self.B("rstd")], W=[self.B("rstd")])
            yv = self.yout.rearrange("(t p) d -> p t d", p=128)
            for t in range(16):
                y = ys[t % 2]
                yb = self.B("ys", t % 2)
                self.op(self.DVE, lambda t=t, y=y: nc.vector.scalar_tensor_tensor(out=y[:], in0=self.x[:, t, :], scalar=self.rstd[:, t:t + 1], in1=fgb[:], op0=ALU.mult, op1=ALU.mult),
                        R=[self.B("x", t), self.B("rstd"), self.B("fgb")], W=[yb])
                self.dma(self.SP, yv[:, t, :], y[:], R=[yb])
            self.barrier()

    def finish(self):
        self.barrier()


def tr_rows_generic(self, pm, row, n, i2):
    for k in range(n):
        last = self.mm(pm[:, 2 * k:2 * k + 2], row[0:2, k * 128:(k + 1) * 128], i2)
    return last


def build_A(even, dbg=False):
    b = Builder("A", even, dbg=dbg)
    with b.es:
        with contextlib.ExitStack() as ps:
            b.prologue(ps)
            b.modulation()
            b.norm_phase(0)
            if dbg:
                b.dma(b.SP, b.dbg_o, b.hT[:], R=[b.B("hT", g) for g in range(5)])
                b.dma(b.SP, b.dbg2[:, 0:NT], b.rstd[:])
                b.dma(b.SP, b.dbg2[:, NT:2 * NT], b.ss[:])
                b.dma(b.SP, b.dbg3, b.ab[:].rearrange("p a k r -> p (a k r)"))
                b.dma(b.SP, b.dbg4, b.modT[:].rearrange("p j r -> p (j r)"))
            b.stage_A_proj()
            b.finish()
    return b


def build_B(even, stop_after=None, dbg=False):
    b = Builder("B", even, dbg=dbg)
    with b.es:
        with contextlib.ExitStack() as ps:
            b.prologue(ps)
            b.modulation()
            b.norm_phase(0)
            with contextlib.ExitStack() as pa:
                b.QT = b.sb(pa, "QT", [128, 8, TOK], BF16)
                b.stage_B_qproj(pa)
                if stop_after == "q":
                    b.dma(b.SP, b.dbg_o, b.QT[:])
                b.attn_scratch(pa)
                if stop_after != "q":
                    if even:
                        import os
                        if not os.environ.get("K_SKIP_DIFF"):
                            b.diff_phase(pa)
                        if not os.environ.get("K_SKIP_NA"):
                            b.na_phase(pa)
                    else:
                        b.swa_phase(pa)
                b.barrier()
            if stop_after == "attn":
                b.dma(b.SP, b.dbg_o, b.hT[:])
            if stop_after is None:
                b.outproj_phase()
                b.norm_phase(1)
                b.ffn_phase()
            b.output_phase()
            b.finish()
    return b


_PROGS = {}


def _prog(kind, even):
    key = (kind, even)
    if key not in _PROGS:
        _PROGS[key] = build_A(even) if kind == "A" else build_B(even)
    return _PROGS[key]


def _na_bias_slot(rpb, G):
    q = np.arange(128)
    r = 2 * G + q // 64
    c = q % 64
    rs = np.clip(r - 4, 0, 248)
    cs = np.clip(c - 8, 0, 48)
    i = np.arange(128)
    out = np.full((128, 8, 7, 128), NEG, np.float32)
    for j in range(7):
        krow = 2 * (G - 3 + j) + i // 64
        kcol = i % 64
        ok = ((krow[:, None] >= 0) & (krow[:, None] < 256)
              & (krow[:, None] >= rs[None, :]) & (krow[:, None] < rs[None, :] + 8)
              & (kcol[:, None] >= cs[None, :]) & (kcol[:, None] < cs[None, :] + 16))
        dr = np.clip(krow[:, None] - r[None, :] + 7, 0, 14)
        dc = np.clip(kcol[:, None] - c[None, :] + 15, 0, 30)
        vals = rpb[:, dr, dc]
        out[:, :, j, :] = np.where(ok[None], vals, np.float32(NEG)).transpose(1, 0, 2)
    return out.reshape(128, 8 * 7 * 128)


def _run(prog, maps):
    res = run_bass_kernel_spmd(prog.nc, maps, core_ids=list(range(NC)))
    return res.results


def kernel(x, c, ctx, c_ctx, ada_w, ada_b, norm_g, w_in_ab, w_out_ab, diff_lambda, diff_sub_g,
           na_rpb, w_in_c, w_out_c, attn_sinks, ffn_w1, ffn_w3, ffn_w2, final_g):
    import math
    f32 = lambda a: np.ascontiguousarray(np.asarray(a, dtype=np.float32))
    x = f32(x)[0]
    ctxa = f32(ctx)[0]
    xs = [np.concatenate([x[r * OWN:(r + 1) * OWN], ctxa], 0) for r in range(NC)]
    ident = np.eye(128, dtype=np.float32)
    sel2 = np.zeros((2, 258), np.float32)
    sel2[0, 0:128] = 1
    sel2[1, 128:256] = 1
    cvec = np.stack([f32(c)[0], f32(c_ctx)])
    cs_tabs = [rope_tables(r) for r in range(NC)]
    res = None
    for L in range(4):
        even = (L % 2 == 0)
        e = L // 2
        if even:
            w_in = f32(w_in_ab[e])
            perm = rot_perm(8)
            w_rot = np.ascontiguousarray(np.concatenate([w_in[:, 0:512][:, perm], w_in[:, 1024:1536][:, perm]], 1))
            w_out = f32(w_out_ab[e])
        else:
            w_in = f32(w_in_c[e])
            w_rot = np.ascontiguousarray(np.concatenate([w_in[:, 0:1024][:, rot_perm(16)], w_in[:, 1024:1280][:, rot_perm(4)]], 1))
            w_out = f32(w_out_c[e])
        base = dict(norm_g=f32(norm_g[L]), ident=ident, sel2=sel2, w_in=w_in, w_rot=w_rot)
        mapsA = [dict(base, xin=xs[r], cvec=cvec, ada_w=f32(ada_w[L]), ada_b=f32(ada_b[L]), cs=cs_tabs[r]) for r in range(NC)]
        ra = _run(_prog("A", even), mapsA)
        kT_g = np.ascontiguousarray(np.stack([ra[r]["kT_o"] for r in range(NC)]))
        v_g = np.ascontiguousarray(np.stack([ra[r]["v_o"] for r in range(NC)]))
        modin = np.ascontiguousarray(ra[0]["mod_o"])
        f0 = 512 if even else 0
        wf = 512 if even else 256
        kfull = np.concatenate([kT_g[r][f0:f0 + wf, :OWN] for r in range(NC)], 1)
        vfull = np.concatenate([v_g[r][:OWN, f0:f0 + wf] for r in range(NC)], 0)
        kpad = np.zeros((wf, SEQ + 768), kfull.dtype)
        kpad[:, 384:384 + SEQ] = kfull
        vpad = np.zeros((SEQ + 768, wf), vfull.dtype)
        vpad[384:384 + SEQ] = vfull
        if even:
            kT_gd = np.ascontiguousarray(kT_g[:, :512, :])
            v_gd = np.ascontiguousarray(v_g[:, :, :512])
            nck = np.ascontiguousarray(kT_g[0, 512:, OWN:])
            ncv = np.ascontiguousarray(v_g[0, OWN:, 512:])
        else:
            kT_gd, v_gd = kT_g, v_g
        mapsB = []
        lam_init = 0.8 - 0.6 * math.exp(-0.3 * L)
        cache = {}
        for r in range(NC):
            m = dict(base, xin=xs[r], modin=modin, cs=cs_tabs[r], kT_g=kT_gd, v_g=v_gd,
                     win_kT=np.ascontiguousarray(kpad[:, r * OWN:r * OWN + WIN * 128]),
                     win_v=np.ascontiguousarray(vpad[r * OWN:r * OWN + WIN * 128]),
                     w_out=w_out, w1=f32(ffn_w1[L]), w3=f32(ffn_w3[L]), w2=f32(ffn_w2[L]), final_g=f32(final_g),
                     lamc=np.tile(np.array([[lam_init, 1.0 - lam_init]], np.float32), (128, 1)))
            if even:
                m["nck"] = nck
                m["ncv"] = ncv
                m["lam"] = f32(diff_lambda[e]).reshape(1, 256)
                m["subg"] = f32(diff_sub_g[e]).reshape(128, 1)
                rpb = f32(na_rpb[e])
                slots = []
                for t in (0, 1, 2, 14, 15):
                    G = 16 * r + t
                    key = G if (G < 2 or G > 125) else -1
                    if key not in cache:
                        cache[key] = _na_bias_slot(rpb, G)
                    slots.append(cache[key])
                m["nab"] = np.ascontiguousarray(np.stack(slots))
            else:
                m["sinks"] = f32(attn_sinks[e]).reshape(1, 16)
                i = np.arange(128)
                prev = np.where(i[:, None] >= i[None, :], 0.0, NEG).astype(np.float32)
                nxt = np.where(i[:, None] <= i[None, :], 0.0, NEG).astype(np.float32)
                allneg = np.full((128, 128), NEG, np.float32)
                m["swam"] = np.ascontiguousarray(np.concatenate([prev, nxt, allneg if r == 0 else prev, allneg if r == NC - 1 else nxt], 1))
            mapsB.append(m)
        res = _run(_prog("B", even), mapsB)
        xs = [np.ascontiguousarray(res[r]["xout"]) for r in range(NC)]
    out = np.concatenate([res[r]["yout"] for r in range(NC)], 0)[None]
    return out.astype(np.float32)
```
